# Optimizing a Trainium2 kernel written in Bass

```python
import jax, jax.numpy as jnp
from jax import lax
import numpy as np

D_MODEL = 1024
BATCH = 2
SEQ = 8192
DEPTH = 4

ATTN_HEADS = 8
ATTN_HEAD_DIM = 64
ATTN_WIDTH = ATTN_HEADS * ATTN_HEAD_DIM
IDX_HEADS = 4
IDX_HEAD_DIM = 64
MAX_TOPK = 256
Q_BLOCK = 128
HGRN_HEADS = 4
HGRN_KEY_DIM = 128
HGRN_VAL_DIM = 128
HGRN_F_WIDTH = HGRN_HEADS * HGRN_KEY_DIM
HGRN_WIDTH = HGRN_HEADS * HGRN_VAL_DIM
HGRN_CHUNK = 64
D_FF = -(-8 * D_MODEL // (3 * 256)) * 256
ROPE_THETA = 10000.0
EPS = 1e-6
IN_WIDTHS = (ATTN_WIDTH, ATTN_WIDTH, ATTN_WIDTH,
             IDX_HEADS * IDX_HEAD_DIM, IDX_HEAD_DIM, IDX_HEADS,
             HGRN_F_WIDTH, HGRN_F_WIDTH, HGRN_WIDTH, HGRN_WIDTH,
             D_MODEL, D_MODEL)
D_IN = sum(IN_WIDTHS)

kernel_name = "hybrid_dsa_hgrn2_gated_block"


def rmsnorm(x, g):
    xf = x.astype(jnp.float32)
    xf = xf * lax.rsqrt(jnp.mean(xf * xf, axis=-1, keepdims=True) + EPS)
    return xf.astype(x.dtype) * g


def rope_tables(positions, dim):
    inv = ROPE_THETA ** (-jnp.arange(0, dim, 2, dtype=jnp.float32) / dim)
    ang = positions.astype(jnp.float32)[..., None] * inv
    return jnp.cos(ang)[:, :, None, :], jnp.sin(ang)[:, :, None, :]


def apply_rope(x, cos, sin):
    half = x.shape[-1] // 2
    x1 = x[..., :half].astype(jnp.float32)
    x2 = x[..., half:].astype(jnp.float32)
    return jnp.concatenate([x1 * cos - x2 * sin, x2 * cos + x1 * sin], axis=-1).astype(x.dtype)


def dsa_attention(q, k, v, qi, ki, wi):
    B, T, H, Dh = q.shape
    topk = min(MAX_TOPK, T // 4)
    nb = T // Q_BLOCK
    scale = Dh ** -0.5
    key_pos = jnp.arange(T)
    ki32 = ki.astype(jnp.float32)
    gather = jax.vmap(lambda a, i: a[i])

    def to_blocks(a):
        return a.reshape((B, nb, Q_BLOCK) + a.shape[2:]).swapaxes(0, 1)

    def block_fn(args):
        qb, qib, wib, start = args
        qpos = start + jnp.arange(Q_BLOCK)
        visible = key_pos[None, :] <= qpos[:, None]
        rel = jax.nn.relu(jnp.einsum('bqhd,bsd->bqhs', qib.astype(jnp.float32), ki32))
        score = jnp.einsum('bqh,bqhs->bqs', wib.astype(jnp.float32), rel)
        score = jnp.where(visible[None], score, -jnp.inf)
        _, idx = lax.top_k(score, topk)
        k_sel = gather(k, idx).astype(jnp.float32)
        v_sel = gather(v, idx).astype(jnp.float32)
        logits = jnp.einsum('bqhd,bqkhd->bhqk', qb.astype(jnp.float32), k_sel) * scale
        valid = idx <= qpos[None, :, None]
        logits = jnp.where(valid[:, None], logits, -jnp.inf)
        p = jax.nn.softmax(logits, axis=-1)
        return jnp.einsum('bhqk,bqkhd->bqhd', p, v_sel).astype(q.dtype)

    starts = jnp.arange(nb) * Q_BLOCK
    out = lax.map(block_fn, (to_blocks(q), to_blocks(qi), to_blocks(wi), starts))
    return out.swapaxes(0, 1).reshape(B, T, H * Dh)


def hgrn2_chunked(q, k, v, log_f):
    B, T, H, dk = q.shape
    dv = v.shape[-1]
    C = HGRN_CHUNK
    nc = T // C

    def chunks(a):
        return a.reshape(B, nc, C, H, a.shape[-1]).transpose(1, 0, 3, 2, 4)

    b = jnp.cumsum(chunks(log_f), axis=3)
    causal = jnp.tril(jnp.ones((C, C), dtype=bool))

    def step(S, xs):
        qc, kc, vc, bc = xs
        inter = jnp.einsum('bhtd,bhde->bhte', qc * jnp.exp(bc), S)
        diff = bc[:, :, :, None, :] - bc[:, :, None, :, :]
        decay = jnp.exp(jnp.where(causal[:, :, None], diff, -jnp.inf))
        A = jnp.einsum('bhtd,bhsd,bhtsd->bhts', qc, kc, decay)
        intra = jnp.einsum('bhts,bhse->bhte', A, vc)
        b_last = bc[:, :, -1:, :]
        S = jnp.exp(b_last[:, :, 0, :])[..., None] * S + jnp.einsum(
            'bhsd,bhse->bhde', kc * jnp.exp(b_last - bc), vc)
        return S, inter + intra

    S0 = jnp.zeros((B, H, dk, dv), jnp.float32)
    _, o = lax.scan(step, S0, (chunks(q), chunks(k), chunks(v), b))
    return o.transpose(1, 0, 3, 2, 4).reshape(B, T, H, dv)


def setup_inputs(seed: int = 0) -> dict:
    key = jax.random.key(seed)
    ks = jax.random.split(key, 16)
    f32 = jnp.float32
    x = jax.random.normal(ks[0], (BATCH, SEQ, D_MODEL), f32)
    offset = jax.random.randint(ks[1], (BATCH, 1), 0, 4096, dtype=jnp.int32)
    positions = offset + jnp.arange(SEQ, dtype=jnp.int32)[None, :]
    res_scale = (2 * DEPTH) ** -0.5
    w_in = jax.random.normal(ks[2], (DEPTH, D_MODEL, D_IN), f32) * D_MODEL ** -0.5
    w_proj_attn = jax.random.normal(ks[3], (DEPTH, ATTN_WIDTH, D_MODEL), f32) * ATTN_WIDTH ** -0.5
    w_proj_hgrn = jax.random.normal(ks[4], (DEPTH, HGRN_WIDTH, D_MODEL), f32) * HGRN_WIDTH ** -0.5
    w_out = jax.random.normal(ks[5], (DEPTH, D_MODEL, D_MODEL), f32) * D_MODEL ** -0.5 * res_scale
    norm_mix = 1.0 + 0.02 * jax.random.normal(ks[6], (DEPTH, D_MODEL), f32)
    norm_ffn = 1.0 + 0.02 * jax.random.normal(ks[7], (DEPTH, D_MODEL), f32)
    q_norm = 1.0 + 0.02 * jax.random.normal(ks[8], (DEPTH, ATTN_HEAD_DIM), f32)
    k_norm = 1.0 + 0.02 * jax.random.normal(ks[9], (DEPTH, ATTN_HEAD_DIM), f32)
    hgrn_norm = 1.0 + 0.02 * jax.random.normal(ks[10], (DEPTH, HGRN_VAL_DIM), f32)
    hgrn_lower_bound = 0.5 * jax.random.normal(ks[11], (DEPTH, HGRN_F_WIDTH), f32)
    w_ffn_in = jax.random.normal(ks[12], (DEPTH, D_MODEL, 2 * D_FF), f32) * D_MODEL ** -0.5
    w_ffn_out = jax.random.normal(ks[13], (DEPTH, D_FF, D_MODEL), f32) * D_FF ** -0.5 * res_scale
    return {"x": x, "positions": positions, "w_in": w_in, "w_proj_attn": w_proj_attn,
            "w_proj_hgrn": w_proj_hgrn, "w_out": w_out, "norm_mix": norm_mix, "norm_ffn": norm_ffn,
            "q_norm": q_norm, "k_norm": k_norm, "hgrn_norm": hgrn_norm,
            "hgrn_lower_bound": hgrn_lower_bound, "w_ffn_in": w_ffn_in, "w_ffn_out": w_ffn_out}


def reference(x, positions, w_in, w_proj_attn, w_proj_hgrn, w_out, norm_mix, norm_ffn,
              q_norm, k_norm, hgrn_norm, hgrn_lower_bound, w_ffn_in, w_ffn_out):
    B, T, _ = x.shape
    cos, sin = rope_tables(positions, ATTN_HEAD_DIM)
    split_points = list(np.cumsum(IN_WIDTHS)[:-1])
    lb_all = jnp.cumsum(jax.nn.softmax(hgrn_lower_bound.astype(jnp.float32), axis=0), axis=0)
    lb_all = lb_all - lb_all[:1]
    for l in range(DEPTH):
        h = rmsnorm(x, norm_mix[l])
        proj = h @ w_in[l]
        aq, ak, av, iq, ik, iw, hq, hf, hi, hg, ga, gb = jnp.split(proj, split_points, axis=-1)
        aq = apply_rope(rmsnorm(aq.reshape(B, T, ATTN_HEADS, ATTN_HEAD_DIM), q_norm[l]), cos, sin)
        ak = apply_rope(rmsnorm(ak.reshape(B, T, ATTN_HEADS, ATTN_HEAD_DIM), k_norm[l]), cos, sin)
        av = av.reshape(B, T, ATTN_HEADS, ATTN_HEAD_DIM)
        iq = apply_rope(iq.reshape(B, T, IDX_HEADS, IDX_HEAD_DIM), cos, sin) * (IDX_HEAD_DIM ** -0.5)
        ik = apply_rope(ik[:, :, None, :], cos, sin)[:, :, 0, :]
        iw = iw * (IDX_HEADS ** -0.5)
        y_attn = dsa_attention(aq, ak, av, iq, ik, iw)
        lb = lb_all[l].reshape(HGRN_HEADS, HGRN_KEY_DIM)
        hf32 = hf.astype(jnp.float32).reshape(B, T, HGRN_HEADS, HGRN_KEY_DIM)
        log_f = jnp.logaddexp(jnp.log(lb), jnp.log1p(-lb) + jax.nn.log_sigmoid(hf32))
        k_in = -jnp.expm1(log_f)
        q_h = jax.nn.silu(hq.astype(jnp.float32)).reshape(B, T, HGRN_HEADS, HGRN_KEY_DIM)
        v_h = hi.astype(jnp.float32).reshape(B, T, HGRN_HEADS, HGRN_VAL_DIM)
        o = hgrn2_chunked(q_h, k_in, v_h, log_f).astype(x.dtype)
        o = rmsnorm(o, hgrn_norm[l]) * jax.nn.silu(hg.reshape(B, T, HGRN_HEADS, HGRN_VAL_DIM))
        y_hgrn = o.reshape(B, T, HGRN_WIDTH)
        merged = jax.nn.sigmoid(ga) * (y_attn @ w_proj_attn[l]) + jax.nn.sigmoid(gb) * (y_hgrn @ w_proj_hgrn[l])
        x = x + merged @ w_out[l]
        h = rmsnorm(x, norm_ffn[l])
        g, u = jnp.split(h @ w_ffn_in[l], 2, axis=-1)
        x = x + (jax.nn.silu(g) * u) @ w_ffn_out[l]
    return x
```

```python
import contextlib
import os
import numpy as np
import concourse.bass as bass
import concourse.mybir as mybir
from concourse.bass_utils import run_bass_kernel_spmd

F32 = mybir.dt.float32
BF16 = mybir.dt.bfloat16
I32 = mybir.dt.int32
AF = mybir.ActivationFunctionType
ALU = mybir.AluOpType
AX = mybir.AxisListType


def _dsize(dt):
    if dt == BF16:
        return 2
    return 4


class _Op:
    __slots__ = ("id", "eng", "fn", "deps", "dma", "sem", "val", "needed")


class Prog:
    ENGS = ("pe", "act", "dve", "pool", "sp")

    def __init__(self, nc, n_dma_sems=16, same_engine_sync=("act", "dve", "pool")):
        self.nc = nc
        self.ops = []
        self.per_eng = {e: [] for e in self.ENGS}
        self.acc = {}
        self.n_dma_sems = n_dma_sems
        self.dma_rr = 0
        self.dma_last = [None] * n_dma_sems
        self.same_engine_sync = set(same_engine_sync)

    @staticmethod
    def region(ap):
        t = ap.tensor
        es = _dsize(ap.dtype)
        pat = [list(x) for x in ap.ap]
        space = str(type(t).__name__)
        if space.startswith("DRam"):
            lo = ap.offset
            hi = lo + sum((c - 1) * abs(s) for s, c in pat)
            return (t.name, 0, 1, lo * es, (hi + 1) * es)
        if space.startswith("PSum"):
            return (t.name, 0, 128, 0, 1 << 30)
        row = 1
        for d in list(t.shape)[1:]:
            row *= int(d)
        p0 = ap.offset // row
        f0 = ap.offset % row
        pstep, pcnt = pat[0]
        if pstep == 0:
            pcnt = 1
        hi = f0 + sum((c - 1) * abs(s) for s, c in pat[1:])
        return (t.name, p0, p0 + pcnt, f0 * es, (hi + 1) * es)

    def _deps_for(self, reg, is_write, opid):
        name, p0, p1, lo, hi = reg
        lst = self.acc.setdefault(name, [])
        deps = set()
        keep = []
        for e in lst:
            ov = not (e[1] <= p0 or p1 <= e[0] or e[3] <= lo or hi <= e[2])
            if ov and (is_write or e[5]):
                if e[4] != opid:
                    deps.add(e[4])
            if is_write and e[4] != opid and p0 <= e[0] and e[1] <= p1 and lo <= e[2] and e[3] <= hi:
                continue
            keep.append(e)
        keep.append([p0, p1, lo, hi, opid, is_write])
        self.acc[name] = keep
        return deps

    def op(self, eng, fn, outs=(), ins=(), dma=False):
        o = _Op()
        o.id = len(self.ops)
        o.eng = eng
        o.fn = fn
        o.dma = dma
        o.sem = None
        o.val = None
        o.needed = False
        deps = set()
        for a in ins:
            if a is None:
                continue
            is_ps = type(a.tensor).__name__.startswith("PSum")
            deps |= self._deps_for(self.region(a), is_ps, o.id)
        for a in outs:
            deps |= self._deps_for(self.region(a), True, o.id)
        if dma:
            k = self.dma_rr % self.n_dma_sems
            self.dma_rr += 1
            if self.dma_last[k] is not None:
                deps.add(self.dma_last[k])
            self.dma_last[k] = o.id
            o.sem = ("dma", k)
        else:
            o.sem = ("eng", eng)
        o.deps = deps
        self.ops.append(o)
        self.per_eng[eng].append(o)
        return o

    def emit(self, es):
        nc = self.nc
        ops = self.ops
        for o in ops:
            for d in o.deps:
                od = ops[d]
                if od.dma or od.eng != o.eng or (o.eng in self.same_engine_sync):
                    od.needed = True
        for o in ops:
            if o.dma:
                o.needed = True
        sems = {}
        for e in self.ENGS:
            sems[("eng", e)] = es.enter_context(nc.semaphore("sem_" + e))
        for k in range(self.n_dma_sems):
            sems[("dma", k)] = es.enter_context(nc.semaphore("sem_dma%d" % k))
        cnt = {k: 0 for k in sems}
        for o in ops:
            if o.needed:
                cnt[o.sem] += 16 if o.dma else 1
                o.val = cnt[o.sem]
        self.final_counts = cnt
        self.sems = sems
        engobj = {"pe": nc.tensor, "act": nc.scalar, "dve": nc.vector, "pool": nc.gpsimd, "sp": nc.sync}
        block = es.enter_context(nc.Block())
        binder = {"pe": block.tensor, "act": block.scalar, "dve": block.vector,
                  "pool": block.gpsimd, "sp": block.sync}
        for e in self.ENGS:
            lst = self.per_eng[e]
            if not lst:
                continue

            def body(engine, lst=lst, e=e):
                known = {}
                for o in lst:
                    need = {}
                    for d in o.deps:
                        od = ops[d]
                        if (not od.dma) and od.eng == e and e not in self.same_engine_sync:
                            continue
                        if od.val > need.get(od.sem, 0):
                            need[od.sem] = od.val
                    for s, v in need.items():
                        if known.get(s, 0) >= v:
                            continue
                        engine.wait_ge(sems[s], v)
                        known[s] = v
                    ins = o.fn(engine)
                    if o.needed:
                        ins.then_inc(sems[o.sem], 16 if o.dma else 1)
                if e == "sp":
                    for k in range(self.n_dma_sems):
                        if cnt[("dma", k)] > 0:
                            engine.wait_ge(sems[("dma", k)], cnt[("dma", k)])

            binder[e](body)


U8 = mybir.dt.uint8
D = 1024
C_AQ, C_AK, C_AV, C_IQ, C_HQ, C_HF, C_HI, C_HG, C_GA, C_GB, C_END = 0, 512, 1024, 1536, 1860, 2372, 2884, 3396, 3908, 4932, 5956
DFF = 2816
NEG = -30000.0
TB_C = 1.0e-6 / 8192.0


def host_consts():
    c = np.zeros((128, 1024), np.float32)
    s = np.arange(128)
    c[:, 0:128] = np.eye(128, dtype=np.float32)
    c[:, 128:256] = (s[:, None] <= s[None, :]).astype(np.float32) - (s[:, None] <= 63).astype(np.float32)
    c[:, 256:384] = (s[:, None] > s[None, :]).astype(np.float32)
    c[:, 384] = (s <= 63)
    c[:, 385] = 1.0
    c[:, 386] = 1e-6
    c[:, 388:392] = np.array([-64.0, -2.5e-6, 0.0, 64.0], np.float32)[None, :]
    c[:, 392:424] = (10000.0 ** (-np.arange(0, 64, 2, dtype=np.float32) / 64)).astype(np.float32)[None, :]
    c[:, 512:640] = (s[:, None] <= s[None, :]).astype(np.float32)
    c[:, 640:768] = np.where(s[None, :] <= s[:, None], 0.0, -1e30).astype(np.float32)
    return c


class Arena:
    def __init__(self, t):
        self.t = t
        self.off = 0

    def a(self, n, shape=None):
        ap = self.t[:, self.off:self.off + n]
        self.off += n
        assert self.off <= int(self.t.shape[1]), (self.off, self.t.shape)
        if shape is not None:
            ap = ap.rearrange("p (a b) -> p a b", a=shape[0])
        return ap


class _Stop(Exception):
    pass


def build_program(T, NL, nbis=26, stop=None, NW=4):
    NB = T // 128
    nc = bass.Bass("TRN2", target_bir_lowering=False)
    dt_in = lambda n, s, d=F32: nc.dram_tensor(n, s, d, kind="ExternalInput").ap()
    x_in = dt_in("x", [T, D])
    pos_in = dt_in("pos", [128, NB], I32)
    w_in = dt_in("w_in", [NW, D, C_END])
    w_pa = dt_in("w_proj_attn", [NW, 512, D])
    w_ph = dt_in("w_proj_hgrn", [NW, 512, D])
    w_o = dt_in("w_out", [NW, D, D])
    n_mix = dt_in("norm_mix", [4, D])
    n_ffn = dt_in("norm_ffn", [4, D])
    qn_in = dt_in("q_norm", [4, 64])
    kn_in = dt_in("k_norm", [4, 64])
    hn_in = dt_in("hgrn_norm", [4, 128])
    hlb_in = dt_in("hgrn_lower_bound", [4, 512])
    w_fi = dt_in("w_ffn_in", [NW, D, 2 * DFF])
    w_fo = dt_in("w_ffn_out", [NW, DFF, D])
    cst_in = dt_in("consts", [128, 1024])
    y_out = nc.dram_tensor("y", [T, D], F32, kind="ExternalOutput").ap()
    dscr = lambda n, s, d: nc.dram_tensor(n, s, d).ap()
    xa = dscr("xa", [T, D], F32)
    xb = dscr("xb", [T, D], F32)
    hT_scr = dscr("hT_scr", [NB, 128, 1024], BF16)
    KT_scr = dscr("KT_scr", [NB, 128, 512], BF16)
    V_scr = dscr("V_scr", [NB, 128, 520], BF16)
    ikT_scr = dscr("ikT_scr", [128, T], BF16)
    qz_scr = dscr("qz_scr", [NB, 128, 1024], BF16)
    iqz_scr = dscr("iqz_scr", [NB, 128, 512], BF16)
    sg_scr = dscr("sg_scr", [NB, 128, 4], F32)
    yh_scr = dscr("yh_scr", [NB, 128, 512], BF16)
    gt_scr = dscr("gt_scr", [NB, 128, 2048], BF16)
    aT_scr = dscr("aT_scr", [NB, 128, 22 * 128], BF16)

    es = contextlib.ExitStack()
    with es:
        P = Prog(nc)
        sb = lambda n, s, d: es.enter_context(nc.sbuf_tensor(n, s, d))
        W = sb("W", [128, 16384], BF16)
        A2 = sb("A2", [128, 24576], BF16)
        FA = sb("FA", [128, max(T, int(os.environ.get("FAMIN", 8192)))], F32)
        FB = sb("FB", [128, 3072], F32)
        JK = sb("JK", [128, T], U8)
        CN = sb("CN", [128, 1024], F32)
        G = sb("G", [128, 3840], F32)
        ID4 = sb("ID4", [128, 512], BF16)
        SS = sb("SS", [128, 512], F32)
        SM = sb("SM", [128, 256], F32)
        PI = sb("PI", [128, NB], I32)
        PF = sb("PF", [128, NB], F32)
        pb = [es.enter_context(nc.psum_tensor("pb%d" % i, [128, 512], F32)) for i in range(8)]

        def mm(out, lhsT, rhs, start, stop):
            P.op("pe", lambda e: e.matmul(out, lhsT=lhsT, rhs=rhs, start=start, stop=stop, skip_group_check=True),
                 outs=[out], ins=[lhsT, rhs])

        def tr(out, in_, ident):
            P.op("pe", lambda e: e.transpose(out=out, in_=in_, identity=ident), outs=[out], ins=[in_, ident])

        def act(out, in_, func, bias=None, scale=None, accum=None):
            kw = {}
            ins = [in_]
            if bias is not None:
                kw["bias"] = bias
                if not isinstance(bias, (int, float)):
                    ins.append(bias)
            if scale is not None:
                kw["scale"] = scale
                if not isinstance(scale, (int, float)):
                    ins.append(scale)
            outs = [out]
            if accum is not None:
                kw["accum_out"] = accum
                outs.append(accum)
            P.op("act", lambda e: e.activation(out=out, in_=in_, func=func, **kw), outs=outs, ins=ins)

        def ts(out, in0, s1, s2, op0, op1=None, accum=None, eng="dve"):
            ins = [in0] + [s for s in (s1, s2) if s is not None and not isinstance(s, (int, float))]
            outs = [out] + ([accum] if accum is not None else [])
            kw = {}
            if op1 is not None:
                kw["op1"] = op1
            if accum is not None:
                kw["accum_out"] = accum
            P.op(eng, lambda e: e.tensor_scalar(out=out, in0=in0, scalar1=s1, scalar2=s2, op0=op0, **kw), outs=outs, ins=ins)

        def stt(out, in0, scalar, in1, op0, op1):
            ins = [in0, in1] + ([] if isinstance(scalar, (int, float)) else [scalar])
            P.op("dve", lambda e: e.scalar_tensor_tensor(out=out, in0=in0, scalar=scalar, in1=in1, op0=op0, op1=op1),
                 outs=[out], ins=ins)

        def tt(out, in0, in1, op, eng="dve"):
            P.op(eng, lambda e: e.tensor_tensor(out=out, in0=in0, in1=in1, op=op), outs=[out], ins=[in0, in1])

        def cp(out, in_, eng="dve"):
            if eng == "act":
                P.op("act", lambda e: e.copy(out=out, in_=in_), outs=[out], ins=[in_])
            else:
                P.op(eng, lambda e: e.tensor_copy(out=out, in_=in_), outs=[out], ins=[in_])

        def dma(out, in_, eng="sp"):
            P.op(eng, lambda e: e.dma_start(out=out, in_=in_), outs=[out], ins=[in_], dma=True)

        def memset(ap, v, eng="pool"):
            P.op(eng, lambda e: e.memset(ap, v), outs=[ap])

        def bc(ap2, n, axis):
            if axis == 1:
                return ap2.unsqueeze(1).to_broadcast([128, n, int(ap2.shape[1])])
            return ap2.unsqueeze(2).to_broadcast([128, int(ap2.shape[1]), n])

        def wload(dst, src_rows, c0, c1):
            nk = int(dst.shape[1])
            for k0 in range(0, nk, 4):
                k1 = min(nk, k0 + 4)
                src = src_rows[k0 * 128:k1 * 128, c0:c1].rearrange("(k p) n -> p k n", p=128)
                dma(dst[:, k0:k1, :], src, eng="pool")

        dma(CN[:], cst_in)
        identf = CN[:, 0:128]
        Mmat = CN[:, 128:256]
        Umat = CN[:, 256:384]
        midones = CN[:, 384:386]
        eps_c = CN[:, 386:387]
        brk = CN[:, 388:392]
        invf = CN[:, 392:424]
        tri01T = CN[:, 512:640]
        tribias = CN[:, 640:768]
        for r in range(4):
            cp(ID4[:, r * 128:(r + 1) * 128], identf)
        dma(PI[:], pos_in)
        cp(PF[:], PI[:])
        LBA = G[:, 1792:3840].rearrange("p (l n) -> p l n", l=4)
        dma(LBA, hlb_in.unsqueeze(0).to_broadcast([128, 4, 512]))
        act(LBA, LBA, AF.Exp)
        sm_den = FB[:, 0:512]
        tt(sm_den, LBA[:, 0, :], LBA[:, 1, :], ALU.add)
        tt(sm_den, sm_den, LBA[:, 2, :], ALU.add)
        tt(sm_den, sm_den, LBA[:, 3, :], ALU.add)
        P.op("dve", lambda e: e.reciprocal(sm_den, sm_den), outs=[sm_den], ins=[sm_den])
        for l in range(4):
            tt(LBA[:, l, :], LBA[:, l, :], sm_den, ALU.mult)
        memset(LBA[:, 0, :], 0.0, eng="dve")
        tt(LBA[:, 2, :], LBA[:, 2, :], LBA[:, 1, :], ALU.add)
        tt(LBA[:, 3, :], LBA[:, 3, :], LBA[:, 2, :], ALU.add)

        def rmsnorm_to_hT(xt, gvec, hTdst, fa):
            junk = fa.a(1024)
            h = fa.a(1024)
            ss = SM[:, 0:1]
            sd = SM[:, 1:2]
            rs = SM[:, 2:3]
            act(junk, xt, AF.Square, accum=ss)
            act(sd, ss, AF.Sqrt, bias=eps_c, scale=1.0 / 1024)
            P.op("dve", lambda e: e.reciprocal(rs, sd), outs=[rs], ins=[sd])
            stt(h, xt, rs, gvec, ALU.mult, ALU.mult)
            for half in range(2):
                for q in range(4):
                    kc = half * 4 + q
                    tr(pb[half][:, q * 128:(q + 1) * 128], h[:, kc * 128:(kc + 1) * 128], identf)
                cp(hTdst[:, half * 4:half * 4 + 4, :], pb[half][:].rearrange("p (a b) -> p a b", a=4), eng="act")

        def rope_tables(i, fa):
            ang = fa.a(32)
            kk = fa.a(32)
            ki = SM[:, 16:48].bitcast(I32) if False else None
            r = fa.a(32)
            c = fa.a(32)
            s = fa.a(32)
            t = fa.a(32)
            ts(ang, invf, PF[:, i:i + 1], None, ALU.mult)
            ts(kk, ang, 1.0 / (2 * np.pi), None, ALU.mult)
            kint = SMI[:, 0:32]
            cp(kint, kk)
            cp(kk, kint)
            stt(r, kk, -6.28125, ang, ALU.mult, ALU.add)
            stt(r, kk, -0.0019353071795864769, r, ALU.mult, ALU.add)

            def wrap(v):
                ts(t, v, np.pi, -2 * np.pi, ALU.is_gt, ALU.mult)
                tt(v, v, t, ALU.add)
                ts(t, v, -np.pi, 2 * np.pi, ALU.is_lt, ALU.mult)
                tt(v, v, t, ALU.add)
            wrap(r)
            act(s, r, AF.Sin)
            ts(c, r, np.pi / 2, None, ALU.add)
            wrap(c)
            act(c, c, AF.Sin)
            return c, s

        def rope(dst, src, nh, c, s, fa):
            t1 = fa.a(nh * 32, (nh, 32))
            t2 = fa.a(nh * 32, (nh, 32))
            cb = bc(c, nh, 1)
            sbb = bc(s, nh, 1)
            x1 = src[:, :, 0:32]
            x2 = src[:, :, 32:64]
            tt(t1, x1, cb, ALU.mult)
            tt(t2, x2, sbb, ALU.mult)
            tt(dst[:, :, 0:32], t1, t2, ALU.subtract)
            tt(t1, x2, cb, ALU.mult)
            tt(t2, x1, sbb, ALU.mult)
            tt(dst[:, :, 32:64], t1, t2, ALU.add)

        SMI = sb("SMI", [128, 64], I32)

        def chk(name):
            if stop == name:
                raise _Stop()

        try:
          chk('consts')
          for l in range(NL):
            x_src = x_in if l == 0 else xb
            x_dst = y_out if l == NL - 1 else xb
            gmix = G[:, 0:1024]
            gffn = G[:, 1024:2048 - 256] if False else None
            dma(G[:, 0:1024], n_mix[l:l + 1, :].to_broadcast([128, 1024]))
            GQ = G[:, 1024:1088]
            GK = G[:, 1088:1152]
            GH = G[:, 1152:1280]
            OML = G[:, 1280:1792]
            dma(GQ, qn_in[l:l + 1, :].to_broadcast([128, 64]))
            dma(GK, kn_in[l:l + 1, :].to_broadcast([128, 64]))
            dma(GH, hn_in[l:l + 1, :].to_broadcast([128, 128]))
            LB = LBA[:, l, :]
            ts(OML, LB, -1.0, 1.0, ALU.mult, ALU.add)

            wa = W[:, 0:8 * 1860].rearrange("p (k n) -> p k n", k=8)
            wload(wa, w_in[l], 0, 1860)
            for i in range(min(NB, int(os.environ.get("NBX", NB)))):
                fa = Arena(FA)
                a2 = Arena(A2)
                xt = fa.a(1024)
                dma(xt, x_src[i * 128:(i + 1) * 128, :])
                hT = a2.a(1024, (8, 128))
                rmsnorm_to_hT(xt, gmix, hT, fa)
                dma(hT_scr[i].rearrange("p (a b) -> p a b", a=8), hT)
                chk('a1')
                cs, sn = rope_tables(i, fa)
                chk('a2')
                bank = 2
                fa_mark = fa.off
                a2_mark = a2.off
                for (c0, wd, kind) in ((0, 512, "q"), (512, 512, "k"), (1024, 512, "v"), (1536, 324, "i")):
                    ps = pb[bank]
                    bank = 2 + (bank - 1) % 4
                    fa.off = fa_mark
                    a2.off = a2_mark
                    for kc in range(8):
                        mm(ps[:, 0:wd], hT[:, kc, :], wa[:, kc, c0:c0 + wd], kc == 0, kc == 7)
                    if kind == 'k':
                        chk('a3')
                    if kind == 'v':
                        chk('a4')
                    if kind == 'i':
                        chk('a5')
                    if kind in ("q", "k"):
                        qf = fa.a(512, (8, 64))
                        qg = fa.a(512, (8, 64))
                        qr = fa.a(512, (8, 64))
                        sq = fa.a(512, (8, 64))
                        cp(qf, ps[:].rearrange("p (a b) -> p a b", a=8), eng="act")
                        tt(qg, qf, bc(GQ if kind == "q" else GK, 8, 1), ALU.mult)
                        tt(sq, qf, qf, ALU.mult)
                        ss8 = SM[:, 8:16]
                        P.op("dve", lambda e, ss8=ss8, sq=sq: e.tensor_reduce(out=ss8, in_=sq, axis=AX.X, op=ALU.add), outs=[ss8], ins=[sq])
                        act(ss8, ss8, AF.Sqrt, bias=eps_c, scale=1.0 / 64)
                        P.op("dve", lambda e, ss8=ss8: e.reciprocal(ss8, ss8), outs=[ss8], ins=[ss8])
                        rope(qr, qg, 8, cs, sn, fa)
                        tt(qr, qr, bc(ss8, 64, 2), ALU.mult)
                        pt = pb[6 if kind == "q" else 7]
                        qr2 = qr.rearrange("p a b -> p (a b)")
                        for p_ in range(4):
                            tr(pt[:, p_ * 128:(p_ + 1) * 128], qr2[:, p_ * 128:(p_ + 1) * 128], identf)
                        if kind == "q":
                            qz = a2.a(1024, (8, 128))
                            memset(qz, 0.0)
                            qz4 = qz.rearrange("p (a two) t -> p a two t", two=2)
                            pt4 = pt[:].rearrange("p (a t) -> p a t", a=4)
                            cp(qz4[0:64, :, 0, :], pt4[0:64, :, :], eng="act")
                            cp(qz4[64:128, :, 1, :], pt4[64:128, :, :], eng="act")
                            dma(qz_scr[i].rearrange("p (a b) -> p a b", a=8), qz)
                        else:
                            kt = a2.a(512, (4, 128))
                            cp(kt, pt[:].rearrange("p (a t) -> p a t", a=4), eng="act")
                            dma(KT_scr[i].rearrange("p (a b) -> p a b", a=4), kt)
                    elif kind == "v":
                        v1 = a2.a(520, (8, 65))
                        memset(v1[:, :, 64:65], 1.0)
                        cp(v1[:, :, 0:64], ps[:].rearrange("p (a b) -> p a b", a=8), eng="act")
                        dma(V_scr[i].rearrange("p (a b) -> p a b", a=8), v1)
                    else:
                        idf = fa.a(324)
                        cp(idf, ps[:, 0:324], eng="act")
                        weff = SM[:, 48:52]
                        sgn = SM[:, 52:56]
                        ts(weff, idf[:, 320:324], 0.0625, None, ALU.mult)
                        act(sgn, idf[:, 320:324], AF.Sign)
                        dma(sg_scr[i], sgn)
                        idr = fa.a(320, (5, 64))
                        rope(idr, idf[:, 0:320].rearrange("p (a b) -> p a b", a=5), 5, cs, sn, fa)
                        tt(idr[:, 0:4, :], idr[:, 0:4, :], bc(weff, 64, 2), ALU.mult)
                        ik2 = fa.a(128)
                        cp(ik2[:, 0:64], idr[:, 4, :])
                        cp(ik2[:, 64:128], idr[:, 4, :])
                        pt = pb[6]
                        idr2 = idr.rearrange("p a b -> p (a b)")
                        tr(pt[:, 0:128], idr2[:, 0:128], identf)
                        tr(pt[:, 128:256], idr2[:, 128:256], identf)
                        tr(pt[:, 256:384], ik2, identf)
                        iqz = a2.a(512, (4, 128))
                        memset(iqz, 0.0)
                        iqz4 = iqz.rearrange("p (a two) t -> p a two t", two=2)
                        pt4 = pt[:, 0:256].rearrange("p (a t) -> p a t", a=2)
                        cp(iqz4[0:64, :, 0, :], pt4[0:64, :, :], eng="act")
                        cp(iqz4[64:128, :, 1, :], pt4[64:128, :, :], eng="act")
                        dma(iqz_scr[i].rearrange("p (a b) -> p a b", a=4), iqz)
                        ikt = a2.a(128)
                        cp(ikt, pt[:, 256:384], eng="act")
                        dma(ikT_scr[:, i * 128:(i + 1) * 128], ikt)

            chk('m1a')
            wh = W[:, 0:8 * 2048].rearrange("p (k n) -> p k n", k=8)
            wload(wh, w_in[l], C_HQ, C_GA)
            memset(SS[:], 0.0, eng="dve")
            for i in range(NB):
                fa = Arena(FA)
                a2 = Arena(A2)
                hT = a2.a(1024, (8, 128))
                dma(hT, hT_scr[i].rearrange("p (a b) -> p a b", a=8))
                pq, pf, pi_, pg = pb[0], pb[1], pb[2], pb[3]
                for ps, c0 in ((pf, 512), (pq, 0), (pi_, 1024), (pg, 1536)):
                    for kc in range(8):
                        mm(ps[:], hT[:, kc, :], wh[:, kc, c0:c0 + 512], kc == 0, kc == 7)
                sg = fa.a(512)
                f = fa.a(512)
                logf = fa.a(512)
                kx = fa.a(512)
                qx = fa.a(512)
                e1 = fa.a(512)
                act(sg, pf[:], AF.Sigmoid)
                tt(f, sg, OML, ALU.mult)
                tt(f, f, LB, ALU.add)
                act(logf, f, AF.Ln)
                ts(kx, f, -1.0, 1.0, ALU.mult, ALU.add)
                act(qx, pq[:], AF.Silu)
                vb = a2.a(512)
                cp(vb, pi_[:], eng="act")
                sgate = fa.a(512)
                act(sgate, pg[:], AF.Silu)
                pc, prv, pcm = pb[4], pb[5], pb[6]
                mm(pc[:], Mmat, logf, True, True)
                mm(prv[:], Umat, logf, True, True)
                for h in range(4):
                    mm(pcm[:, 2 * h:2 * h + 2], logf[:, h * 128:(h + 1) * 128], midones, h == 0, h == 3)
                ecm = SM[:, 64:72]
                act(ecm, pcm[:, 0:8], AF.Exp)
                act(e1, pc[:], AF.Exp)
                qt = fa.a(512)
                tt(qt, qx, e1, ALU.mult)
                act(e1, pc[:], AF.Exp, scale=-1.0)
                ktl = fa.a(512)
                tt(ktl, kx, e1, ALU.mult)
                act(e1, prv[:], AF.Exp)
                khat = a2.a(512)
                tt(khat, kx, e1, ALU.mult)
                pqt, pkt = pb[7], pb[4]
                for h in range(4):
                    tr(pqt[:, h * 128:(h + 1) * 128], qt[:, h * 128:(h + 1) * 128], identf)
                qtT = a2.a(512, (4, 128))
                cp(qtT, pqt[:].rearrange("p (a b) -> p a b", a=4), eng="act")
                for h in range(4):
                    tr(pkt[:, h * 128:(h + 1) * 128], ktl[:, h * 128:(h + 1) * 128], identf)
                ktT = a2.a(512, (4, 128))
                cp(ktT, pkt[:].rearrange("p (a b) -> p a b", a=4), eng="act")
                pA = pb[5]
                klo = a2.a(512, (4, 128))
                khi = a2.a(512, (4, 128))
                qhi = a2.a(512, (4, 128))
                memset(klo[:, :, 64:128], 0.0)
                memset(khi[:, :, 0:64], 0.0)
                memset(qhi[:, :, 0:64], 0.0)
                cp(klo[:, :, 0:64], ktT[:, :, 0:64], eng="pool")
                cp(khi[:, :, 64:128], ktT[:, :, 64:128], eng="pool")
                cp(qhi[:, :, 64:128], qtT[:, :, 64:128], eng="pool")
                for h in range(4):
                    mm(pA[:, h * 128:(h + 1) * 128], klo[:, h, :], qtT[:, h, :], h == 0, False)
                    mm(pA[:, h * 128:(h + 1) * 128], khi[:, h, :], qhi[:, h, :], False, h == 3)
                AT = a2.a(512, (4, 128))
                tt(AT, pA[:].rearrange("p (a b) -> p a b", a=4), bc(tri01T, 4, 1), ALU.mult)
                Sb = a2.a(512, (4, 128))
                SS3 = SS[:].rearrange("p (a b) -> p a b", a=4)
                ecm3 = ecm.rearrange("p (a two) -> p a two", two=2)
                tt(Sb, SS3, ecm3[:, :, 0:1].to_broadcast([128, 4, 128]), ALU.mult)
                pO = pb[6]
                for h in range(4):
                    mm(pO[:, h * 128:(h + 1) * 128], qtT[:, h, :], Sb[:, h, :], h == 0, False)
                    mm(pO[:, h * 128:(h + 1) * 128], AT[:, h, :], vb[:, h * 128:(h + 1) * 128], False, h == 3)
                pL = pb[7]
                for h in range(4):
                    mm(pL[:, h * 128:(h + 1) * 128], khat[:, h * 128:(h + 1) * 128], vb[:, h * 128:(h + 1) * 128], h == 0, h == 3)
                tt(SS3, SS3, ecm3[:, :, 1:2].to_broadcast([128, 4, 128]), ALU.mult)
                tt(SS[:], SS[:], pL[:], ALU.add)
                of = fa.a(512, (4, 128))
                cp(of, pO[:].rearrange("p (a b) -> p a b", a=4), eng="act")
                sq = fa.a(512, (4, 128))
                tt(sq, of, of, ALU.mult)
                ss4 = SM[:, 72:76]
                P.op("dve", lambda e, ss4=ss4, sq=sq: e.tensor_reduce(out=ss4, in_=sq, axis=AX.X, op=ALU.add), outs=[ss4], ins=[sq])
                act(ss4, ss4, AF.Sqrt, bias=eps_c, scale=1.0 / 128)
                P.op("dve", lambda e, ss4=ss4: e.reciprocal(ss4, ss4), outs=[ss4], ins=[ss4])
                tt(of, of, bc(ss4, 128, 2), ALU.mult)
                tt(of, of, bc(GH, 4, 1), ALU.mult)
                of2 = of.rearrange("p a b -> p (a b)")
                tt(of2, of2, sgate, ALU.mult)
                pt = pb[0]
                for h in range(4):
                    tr(pt[:, h * 128:(h + 1) * 128], of2[:, h * 128:(h + 1) * 128], identf)
                yhT = a2.a(512)
                cp(yhT, pt[:], eng="act")
                dma(yh_scr[i], yhT)

            chk('m1b')
            wg_ = W[:, 0:8 * 2048].rearrange("p (k n) -> p k n", k=8)
            wload(wg_, w_in[l], C_GA, C_END)
            for i in range(NB):
                a2 = Arena(A2)
                hT = a2.a(1024, (8, 128))
                dma(hT, hT_scr[i].rearrange("p (a b) -> p a b", a=8))
                gts = a2.a(2048)
                for c in range(4):
                    ps = pb[c]
                    for kc in range(8):
                        mm(ps[:], hT[:, kc, :], wg_[:, kc, c * 512:(c + 1) * 512], kc == 0, kc == 7)
                    act(gts[:, c * 512:(c + 1) * 512], ps[:], AF.Sigmoid)
                dma(gt_scr[i], gts)

            chk('m1c')
            wpa = W[:, 0:4096].rearrange("p (k n) -> p k n", k=4)
            wph = W[:, 4096:8192].rearrange("p (k n) -> p k n", k=4)
            wo = W[:, 8192:16384].rearrange("p (k n) -> p k n", k=8)
            wload(wpa, w_pa[l], 0, 1024)
            wload(wph, w_ph[l], 0, 1024)
            wload(wo, w_o[l], 0, 1024)
            a2 = Arena(A2)
            maskb = a2.a(T)
            ikc = [a2.a(512) for _ in range(2)]
            ktc = [a2.a(2048, (4, 512)) for _ in range(2)]
            vc = [a2.a(2080, (4, 520)) for _ in range(2)]
            pex = [a2.a(512, (4, 128)) for _ in range(2)]
            qz = a2.a(1024, (8, 128))
            iqz = a2.a(512, (4, 128))
            yaT = a2.a(512, (4, 128))
            yhT = a2.a(512, (4, 128))
            gts = a2.a(2048)
            mgT = a2.a(1024, (8, 128))
            score = FA[:, 0:T]
            for i in range(NB):
                N = (i + 1) * 128
                fb = Arena(FB)
                dma(iqz, iqz_scr[i].rearrange("p (a b) -> p a b", a=4))
                dma(qz, qz_scr[i].rearrange("p (a b) -> p a b", a=8))
                sgn = SM[:, 80:84]
                dma(sgn, sg_scr[i])
                for c0 in range(0, N, 512):
                    wdt = min(512, N - c0)
                    sc = score[:, c0:c0 + wdt]
                    ikx = ikc[(c0 // 512) % 2]
                    dma(ikx[:, 0:wdt], ikT_scr[:, c0:c0 + wdt])
                    P.op("pool", lambda e, sc=sc, c0=c0, wdt=wdt: e.iota(sc, pattern=[[1, wdt]], base=c0, channel_multiplier=0,
                                                                         allow_small_or_imprecise_dtypes=True), outs=[sc])
                    ts(sc, sc, -TB_C, -1.0e-6, ALU.mult, ALU.add, eng="pool")
                    for h in range(4):
                        ps = pb[h]
                        mm(ps[:, 0:wdt], iqz[:, h, :], ikx[:, 0:wdt], True, True)
                        act(ps[:, 0:wdt], ps[:, 0:wdt], AF.Relu, scale=sgn[:, h:h + 1])
                        stt(sc, ps[:, 0:wdt], sgn[:, h:h + 1], sc, ALU.mult, ALU.add)
                tt(score[:, i * 128:N], score[:, i * 128:N], tribias, ALU.add)
                sv = score[:, 0:N]
                jk = JK[:, 0:N]
                lo = SM[:, 84:85]
                hi = SM[:, 85:86]
                mid = SM[:, 86:87]
                cnt = SM[:, 87:88]
                g0 = SMI[:, 32:33]
                g1 = SMI[:, 33:34]
                ts(jk, sv, 0.0, None, ALU.is_ge, ALU.add, accum=cnt)
                ts(g0, cnt, 255.5, None, ALU.is_ge)
                ts(jk, sv, -2.5e-6, None, ALU.is_ge, ALU.add, accum=cnt)
                ts(g1, cnt, 255.5, None, ALU.is_ge)
                cp(lo, brk[:, 0:1])
                cp(hi, brk[:, 1:2])
                P.op("dve", lambda e: e.copy_predicated(out=lo, mask=g1, data=brk[:, 1:2]), outs=[lo], ins=[g1, brk[:, 1:2], lo])
                P.op("dve", lambda e: e.copy_predicated(out=hi, mask=g1, data=brk[:, 2:3]), outs=[hi], ins=[g1, brk[:, 2:3], hi])
                P.op("dve", lambda e: e.copy_predicated(out=lo, mask=g0, data=brk[:, 2:3]), outs=[lo], ins=[g0, brk[:, 2:3], lo])
                P.op("dve", lambda e: e.copy_predicated(out=hi, mask=g0, data=brk[:, 3:4]), outs=[hi], ins=[g0, brk[:, 3:4], hi])
                ge = SMI[:, 34:35]
                lt = SMI[:, 35:36]
                for it in range(nbis):
                    ts(mid, lo, hi, 0.5, ALU.add, ALU.mult)
                    ts(jk, sv, mid, None, ALU.is_ge, ALU.add, accum=cnt)
                    ts(ge, cnt, 255.5, None, ALU.is_ge)
                    ts(lt, cnt, 255.5, None, ALU.is_lt)
                    P.op("dve", lambda e: e.copy_predicated(out=lo, mask=ge, data=mid), outs=[lo], ins=[ge, mid, lo])
                    P.op("dve", lambda e: e.copy_predicated(out=hi, mask=lt, data=mid), outs=[hi], ins=[lt, mid, hi])
                ts(maskb[:, 0:N], sv, lo, NEG, ALU.is_lt, ALU.mult)
                pO = [pb[6], pb[7]]
                ntile = i + 1
                for g0_ in range(0, ntile, 4):
                    gn = min(4, ntile - g0_)
                    gi = (g0_ // 4) % 2
                    dma(ktc[gi][:, 0:gn, :], KT_scr[g0_:g0_ + gn].rearrange("a p n -> p a n"))
                    dma(vc[gi][:, 0:gn, :], V_scr[g0_:g0_ + gn].rearrange("a p n -> p a n"))
                    for tl in range(gn):
                        j = g0_ + tl
                        for half in range(2):
                            pl = pb[4 + half]
                            mm(pl[:], maskb[:, j * 128:(j + 1) * 128], ID4[:], True, False)
                            for hh in range(4):
                                hd = half * 4 + hh
                                mm(pl[:, hh * 128:(hh + 1) * 128], ktc[gi][:, tl, (hd // 2) * 128:(hd // 2 + 1) * 128], qz[:, hd, :], False, hh == 3)
                            px = pex[half]
                            act(px, pl[:].rearrange("p (a b) -> p a b", a=4), AF.Exp, scale=0.125)
                            for hh in range(4):
                                hd = half * 4 + hh
                                mm(pO[half][:, hh * 65:(hh + 1) * 65], px[:, hh, :], vc[gi][:, tl, hd * 65:(hd + 1) * 65],
                                   (j == 0 and hh == 0), (j == ntile - 1 and hh == 3))
                yat = fb.a(512, (8, 64))
                rc = SM[:, 96:104]
                for half in range(2):
                    o3 = pO[half][:, 0:260].rearrange("p (a b) -> p a b", a=4)
                    cp(rc[:, half * 4:half * 4 + 4], o3[:, :, 64])
                P.op("dve", lambda e: e.reciprocal(rc, rc), outs=[rc], ins=[rc])
                for half in range(2):
                    o3 = pO[half][:, 0:260].rearrange("p (a b) -> p a b", a=4)
                    tt(yat[:, half * 4:half * 4 + 4, :], o3[:, :, 0:64], bc(rc[:, half * 4:half * 4 + 4], 64, 2), ALU.mult)
                yat2 = yat.rearrange("p a b -> p (a b)")
                pt = pb[0]
                for c in range(4):
                    tr(pt[:, c * 128:(c + 1) * 128], yat2[:, c * 128:(c + 1) * 128], identf)
                cp(yaT, pt[:].rearrange("p (a b) -> p a b", a=4), eng="act")
                dma(yhT, yh_scr[i].rearrange("p (a b) -> p a b", a=4))
                dma(gts, gt_scr[i])
                mg = fb.a(1024)
                m2 = fb.a(512)
                for hf in range(2):
                    pa_, ph_ = pb[1], pb[2]
                    for c in range(4):
                        mm(pa_[:], yaT[:, c, :], wpa[:, c, hf * 512:(hf + 1) * 512], c == 0, c == 3)
                    for c in range(4):
                        mm(ph_[:], yhT[:, c, :], wph[:, c, hf * 512:(hf + 1) * 512], c == 0, c == 3)
                    tt(mg[:, hf * 512:(hf + 1) * 512], pa_[:], gts[:, hf * 512:(hf + 1) * 512], ALU.mult)
                    tt(m2, ph_[:], gts[:, 1024 + hf * 512:1024 + (hf + 1) * 512], ALU.mult)
                    tt(mg[:, hf * 512:(hf + 1) * 512], mg[:, hf * 512:(hf + 1) * 512], m2, ALU.add)
                for half in range(2):
                    for q in range(4):
                        kc = half * 4 + q
                        tr(pb[1 + half][:, q * 128:(q + 1) * 128], mg[:, kc * 128:(kc + 1) * 128], identf)
                    cp(mgT[:, half * 4:half * 4 + 4, :], pb[1 + half][:].rearrange("p (a b) -> p a b", a=4), eng="act")
                xt = fb.a(1024)
                dma(xt, x_src[i * 128:(i + 1) * 128, :])
                for hf in range(2):
                    pd = pb[3]
                    for kc in range(8):
                        mm(pd[:], mgT[:, kc, :], wo[:, kc, hf * 512:(hf + 1) * 512], kc == 0, kc == 7)
                    tt(xt[:, hf * 512:(hf + 1) * 512], xt[:, hf * 512:(hf + 1) * 512], pd[:], ALU.add)
                dma(xa[i * 128:(i + 1) * 128, :], xt)

            chk('m2')
            dma(G[:, 0:1024], n_ffn[l:l + 1, :].to_broadcast([128, 1024]))
            c_base = 0
            for nch in (8, 8, 6):
                wgp = W[:, 0:8 * nch * 128].rearrange("p (k n) -> p k n", k=8)
                wup = W[:, 8192:8192 + 8 * nch * 128].rearrange("p (k n) -> p k n", k=8)
                wload(wgp, w_fi[l], c_base * 128, (c_base + nch) * 128)
                wload(wup, w_fi[l], DFF + c_base * 128, DFF + (c_base + nch) * 128)
                for i in range(NB):
                    fa = Arena(FA)
                    a2 = Arena(A2)
                    hT = a2.a(1024, (8, 128))
                    if c_base == 0:
                        xt = fa.a(1024)
                        dma(xt, xa[i * 128:(i + 1) * 128, :])
                        rmsnorm_to_hT(xt, G[:, 0:1024], hT, fa)
                        dma(hT_scr[i].rearrange("p (a b) -> p a b", a=8), hT)
                    else:
                        dma(hT, hT_scr[i].rearrange("p (a b) -> p a b", a=8))
                    aT = a2.a(nch * 128)
                    for q in range((nch + 3) // 4):
                        nf = min(4, nch - q * 4)
                        pg_, pu_ = pb[2 + 2 * (q % 2)], pb[3 + 2 * (q % 2)]
                        for r in range(nf):
                            fc = q * 4 + r
                            for kc in range(8):
                                mm(pg_[:, r * 128:(r + 1) * 128], wgp[:, kc, fc * 128:(fc + 1) * 128], hT[:, kc, :], (r == 0 and kc == 0), (r == nf - 1 and kc == 7))
                        for r in range(nf):
                            fc = q * 4 + r
                            for kc in range(8):
                                mm(pu_[:, r * 128:(r + 1) * 128], wup[:, kc, fc * 128:(fc + 1) * 128], hT[:, kc, :], (r == 0 and kc == 0), (r == nf - 1 and kc == 7))
                        sg = fa.a(512)
                        act(sg[:, 0:nf * 128], pg_[:, 0:nf * 128], AF.Silu)
                        tt(aT[:, q * 512:q * 512 + nf * 128], sg[:, 0:nf * 128], pu_[:, 0:nf * 128], ALU.mult)
                    dma(aT_scr[i][:, c_base * 128:(c_base + nch) * 128], aT)
                c_base += nch
            chk('f1')
            for hf in range(2):
                wfo = W[:, 0:11264].rearrange("p (k n) -> p k n", k=22)
                for k0 in range(0, 22, 4):
                    k1 = min(22, k0 + 4)
                    dma(wfo[:, k0:k1, :], w_fo[l][k0 * 128:k1 * 128, hf * 512:(hf + 1) * 512].rearrange("(k p) n -> p k n", p=128), eng="pool")
                for i in range(NB):
                    fa = Arena(FA)
                    a2 = Arena(A2)
                    aT = a2.a(2816, (22, 128))
                    dma(aT, aT_scr[i].rearrange("p (a b) -> p a b", a=22))
                    xt = fa.a(512)
                    dma(xt, xa[i * 128:(i + 1) * 128, hf * 512:(hf + 1) * 512])
                    pd = pb[hf]
                    for fc in range(22):
                        mm(pd[:], aT[:, fc, :], wfo[:, fc, :], fc == 0, fc == 21)
                    tt(xt, xt, pd[:], ALU.add)
                    dma(x_dst[i * 128:(i + 1) * 128, hf * 512:(hf + 1) * 512], xt)

        except _Stop:
            t_ = FB[:, 0:1024]
            dma(t_, x_in[0:128, :])
            dma(y_out[0:128, :], t_)
        P.emit(es)
    return nc


_CACHE = {}


def kernel(x, positions, w_in, w_proj_attn, w_proj_hgrn, w_out, norm_mix, norm_ffn, q_norm, k_norm,
           hgrn_norm, hgrn_lower_bound, w_ffn_in, w_ffn_out):
    B, T, _ = x.shape
    NL = w_in.shape[0]
    key = (T, NL)
    if key not in _CACHE:
        _CACHE[key] = build_program(T, NL)
    nc = _CACHE[key]
    f = lambda a: np.ascontiguousarray(np.asarray(a, dtype=np.float32))
    shared = {"w_in": f(w_in), "w_proj_attn": f(w_proj_attn), "w_proj_hgrn": f(w_proj_hgrn), "w_out": f(w_out),
              "norm_mix": f(norm_mix), "norm_ffn": f(norm_ffn), "q_norm": f(q_norm), "k_norm": f(k_norm),
              "hgrn_norm": f(hgrn_norm), "hgrn_lower_bound": f(hgrn_lower_bound), "w_ffn_in": f(w_ffn_in),
              "w_ffn_out": f(w_ffn_out), "consts": host_consts()}
    x = np.asarray(x, dtype=np.float32)
    positions = np.asarray(positions).astype(np.int32)
    in_maps = []
    for c in range(B):
        b = c % B
        m = dict(shared)
        m["x"] = np.ascontiguousarray(x[b])
        m["pos"] = np.ascontiguousarray(positions[b].reshape(T // 128, 128).T)
        in_maps.append(m)
    res = run_bass_kernel_spmd(nc, in_maps, core_ids=list(range(B)))
    return np.stack([np.asarray(res.results[b]["y"], dtype=np.float32) for b in range(B)], axis=0)
```
